# Optimizing a Trainium2 kernel written in Bass

```python
import jax, jax.numpy as jnp
from jax import lax
import numpy as np

D_MODEL = 1024
BATCH = 2
SEQ = 16384
DEPTH = 1

PLE_DIM = 256
POOL_GROUPS = 4
POOL_WIDTH = D_MODEL // 2
POOL_GROUP_DIM = POOL_WIDTH // POOL_GROUPS
POOL_WINDOWS = (2, 4, 8, 16)
MAX_WINDOW = 16
LRU_WIDTH = D_MODEL
LRU_HEADS = 8
LRU_HEAD_DIM = LRU_WIDTH // LRU_HEADS
CONV_WIDTH = 4
LRU_C = 8.0
N_BRANCHES = 2
D_FF = ((8 * D_MODEL // 3 + 255) // 256) * 256
RMS_EPS = 1e-6
IN_COLS = POOL_WIDTH + 2 * LRU_WIDTH + N_BRANCHES * D_MODEL

kernel_name = "hybrid_pool_rglru_gated_block"


def rms_norm(x, g):
    xf = x.astype(jnp.float32)
    return xf * lax.rsqrt(jnp.mean(xf * xf, axis=-1, keepdims=True) + RMS_EPS) * g.astype(jnp.float32)


def causal_multiscale_pool(z, w_grp, scale):
    B, S, _ = z.shape
    zf = z.astype(jnp.float32)
    csum = jnp.cumsum(zf, axis=1)
    csum_pad = jnp.pad(csum, ((0, 0), (MAX_WINDOW, 0), (0, 0)))
    pos = jnp.arange(S)
    outs = []
    for g, w in enumerate(POOL_WINDOWS):
        lo, hi = g * POOL_GROUP_DIM, (g + 1) * POOL_GROUP_DIM
        cur = csum[:, :, lo:hi]
        prev = csum_pad[:, MAX_WINDOW - w:MAX_WINDOW - w + S, lo:hi]
        count = jnp.minimum(pos + 1, w).astype(jnp.float32)[None, :, None]
        outs.append((cur - prev) / count - zf[:, :, lo:hi])
    pooled = jnp.stack(outs, axis=2)
    mixed = jnp.einsum('bsgc,gcd->bsgd', pooled, w_grp.astype(jnp.float32))
    return mixed.reshape(B, S, POOL_WIDTH) * scale.astype(jnp.float32)


def _linear_recurrence_combine(earlier, later):
    a1, b1 = earlier
    a2, b2 = later
    return a1 * a2, a2 * b1 + b2


def rglru_branch(z_x, z_g, conv_w, conv_b, w_rg, b_rg, w_ig, b_ig, lam):
    B, S, _ = z_x.shape
    f32 = jnp.float32
    xc = lax.conv_general_dilated(
        z_x.astype(f32), conv_w.astype(f32)[:, None, :], window_strides=(1,),
        padding=[(CONV_WIDTH - 1, 0)], dimension_numbers=('NWC', 'WIO', 'NWC'),
        feature_group_count=LRU_WIDTH) + conv_b.astype(f32)
    xh = xc.reshape(B, S, LRU_HEADS, LRU_HEAD_DIM)
    r = jax.nn.sigmoid(jnp.einsum('bshi,hij->bshj', xh, w_rg.astype(f32)) + b_rg.astype(f32)).reshape(B, S, LRU_WIDTH)
    ig = jax.nn.sigmoid(jnp.einsum('bshi,hij->bshj', xh, w_ig.astype(f32)) + b_ig.astype(f32)).reshape(B, S, LRU_WIDTH)
    log_a = -LRU_C * r * jax.nn.softplus(-lam.astype(f32))
    a = jnp.exp(log_a)
    mult = jnp.sqrt(jnp.maximum(1.0 - jnp.exp(2.0 * log_a), 0.0))
    mult = jnp.where((jnp.arange(S) == 0)[None, :, None], 1.0, mult)
    b = mult * ig * xc
    _, h = lax.associative_scan(_linear_recurrence_combine, (a, b), axis=1)
    return h * jax.nn.gelu(z_g.astype(f32))


def setup_inputs(seed: int = 0) -> dict:
    key = jax.random.key(seed)
    ks = jax.random.split(key, 32)
    f32 = jnp.float32
    nrm = lambda k, shape, fan_in: jax.random.normal(k, shape, f32) * (fan_in ** -0.5)
    gain = lambda k, shape: 1.0 + 0.02 * jax.random.normal(k, shape, f32)
    small = lambda k, shape: 0.02 * jax.random.normal(k, shape, f32)
    u = jax.random.uniform(ks[14], (DEPTH, LRU_WIDTH), f32, 0.9, 0.999)
    s = u ** (1.0 / LRU_C)
    lru_lambda = jnp.log(s) - jnp.log1p(-s)
    return {
        "x": jax.random.normal(ks[0], (BATCH, SEQ, D_MODEL), f32),
        "p": jax.random.normal(ks[1], (DEPTH, BATCH, SEQ, PLE_DIM), f32),
        "norm1_g": gain(ks[2], (DEPTH, D_MODEL)),
        "w_in": nrm(ks[3], (DEPTH, D_MODEL, IN_COLS), D_MODEL),
        "b_gate": small(ks[4], (DEPTH, N_BRANCHES, D_MODEL)),
        "pool_w": nrm(ks[5], (DEPTH, POOL_GROUPS, POOL_GROUP_DIM, POOL_GROUP_DIM), POOL_GROUP_DIM),
        "pool_scale": 1.0 + 0.1 * jax.random.normal(ks[6], (DEPTH, POOL_WIDTH), f32),
        "pool_proj": nrm(ks[7], (DEPTH, POOL_WIDTH, D_MODEL), POOL_WIDTH),
        "conv_w": nrm(ks[8], (DEPTH, CONV_WIDTH, LRU_WIDTH), CONV_WIDTH),
        "conv_b": small(ks[9], (DEPTH, LRU_WIDTH)),
        "w_rg": nrm(ks[10], (DEPTH, LRU_HEADS, LRU_HEAD_DIM, LRU_HEAD_DIM), LRU_HEAD_DIM),
        "b_rg": small(ks[11], (DEPTH, LRU_HEADS, LRU_HEAD_DIM)),
        "w_ig": nrm(ks[12], (DEPTH, LRU_HEADS, LRU_HEAD_DIM, LRU_HEAD_DIM), LRU_HEAD_DIM),
        "b_ig": small(ks[13], (DEPTH, LRU_HEADS, LRU_HEAD_DIM)),
        "lru_lambda": lru_lambda,
        "lru_proj": nrm(ks[15], (DEPTH, LRU_WIDTH, D_MODEL), LRU_WIDTH),
        "w_out": nrm(ks[16], (DEPTH, D_MODEL, D_MODEL), D_MODEL),
        "norm2_g": gain(ks[17], (DEPTH, D_MODEL)),
        "w_ffn_in": nrm(ks[18], (DEPTH, D_MODEL, 2 * D_FF), D_MODEL),
        "w_ffn_out": nrm(ks[19], (DEPTH, D_FF, D_MODEL), D_FF),
        "ple_norm_g": gain(ks[20], (DEPTH, D_MODEL)),
        "w_ple_gate": nrm(ks[21], (DEPTH, D_MODEL, D_MODEL), D_MODEL),
        "w_ple_proj": nrm(ks[22], (DEPTH, PLE_DIM, D_MODEL), PLE_DIM),
        "final_g": gain(ks[23], (D_MODEL,)),
    }


def reference(x, p, norm1_g, w_in, b_gate, pool_w, pool_scale, pool_proj, conv_w, conv_b,
              w_rg, b_rg, w_ig, b_ig, lru_lambda, lru_proj, w_out, norm2_g, w_ffn_in,
              w_ffn_out, ple_norm_g, w_ple_gate, w_ple_proj, final_g):
    B, S, _ = x.shape
    f32 = jnp.float32
    h = x.astype(f32)
    for i in range(DEPTH):
        u = rms_norm(h, norm1_g[i])
        z = u @ w_in[i].astype(f32)
        z_pool, z_lru, z_gelu, z_gate = jnp.split(
            z, [POOL_WIDTH, POOL_WIDTH + LRU_WIDTH, POOL_WIDTH + 2 * LRU_WIDTH], axis=-1)
        y_pool = causal_multiscale_pool(z_pool, pool_w[i], pool_scale[i]) @ pool_proj[i].astype(f32)
        y_lru = rglru_branch(z_lru, z_gelu, conv_w[i], conv_b[i], w_rg[i], b_rg[i],
                             w_ig[i], b_ig[i], lru_lambda[i]) @ lru_proj[i].astype(f32)
        gates = jax.nn.sigmoid(z_gate.reshape(B, S, N_BRANCHES, D_MODEL) + b_gate[i].astype(f32))
        merged = gates[:, :, 0, :] * y_pool + gates[:, :, 1, :] * y_lru
        h = h + merged @ w_out[i].astype(f32)
        v = rms_norm(h, norm2_g[i])
        g_ff, u_ff = jnp.split(v @ w_ffn_in[i].astype(f32), 2, axis=-1)
        h = h + (jax.nn.silu(g_ff) * u_ff) @ w_ffn_out[i].astype(f32)
        e = p[i].astype(f32) @ w_ple_proj[i].astype(f32)
        ple_gate = jax.nn.sigmoid(rms_norm(h, ple_norm_g[i]) @ w_ple_gate[i].astype(f32))
        h = h + ple_gate * e
    return rms_norm(h, final_g).astype(x.dtype)
```

```python
import numpy as np
from collections import deque
from contextlib import ExitStack

import concourse.bass as bass
import concourse.mybir as mybir
from concourse.bass_utils import run_bass_kernel_spmd

F32 = mybir.dt.float32
BF16 = mybir.dt.bfloat16
AF = mybir.ActivationFunctionType
ALU = mybir.AluOpType

NCORES = 8
D = 1024
SEQ = 16384
TPC = 4096
NB = 512
NBLK = TPC // NB
HALO = 16
WB = NB + HALO
DFF = 2816
NF = DFF // 128
UC = 4096
NU = 36
RING = 5
EPS = 1e-6

V_G1, V_G2, V_G3, V_GF = 0, 8, 16, 24
V_BG0, V_BG1 = 32, 40
V_PSC = 48
V_CW = 52
V_BRG, V_BIG = 84, 92
V_LAM = 100
NV = 108
X_HBG0, X_HBG1, X_HBRG, X_HBIG = 0, 8, 16, 24
X_CH, X_CF, X_CH4 = 32, 40, 48
X_EPS = 56
NX = 57
PC_FLAG, PC_OMF, PC_M, PC_OMM = 0, 1, 2, 10
NPC = 18

ENGINES = ("tensor", "vector", "scalar", "gpsimd", "sync")


class Buf:
    __slots__ = ("name", "w", "r")

    def __init__(self, name):
        self.name = name
        self.w = None
        self.r = []


class Instr:
    __slots__ = ("eng", "idx", "fn", "deps", "mark", "semval", "dma", "waits", "inc")

    def __init__(self, eng, idx, fn, deps, dma, inc):
        self.eng = eng
        self.idx = idx
        self.fn = fn
        self.deps = deps
        self.mark = False
        self.semval = 0
        self.dma = dma
        self.waits = []
        self.inc = inc


class Graph:
    def __init__(self, same_engine_sync=True):
        self.q = {e: [] for e in ENGINES}
        self.dma_cnt = {}
        self.same_engine_sync = same_engine_sync

    def add(self, eng, fn, reads=(), writes=(), acc=False, dma_sem=None, inc=16):
        deps = []
        seen = set()

        def push(d):
            if d is not None and id(d) not in seen:
                seen.add(id(d))
                deps.append(d)

        for b in reads:
            push(b.w)
        for b in writes:
            if not acc:
                push(b.w)
                for r in b.r:
                    push(r)
        dma = None
        if dma_sem is not None:
            self.dma_cnt[dma_sem] = self.dma_cnt.get(dma_sem, 0) + inc
            dma = (dma_sem, self.dma_cnt[dma_sem])
        ins = Instr(eng, len(self.q[eng]), fn, deps, dma, inc)
        self.q[eng].append(ins)
        for b in reads:
            b.r.append(ins)
        for b in writes:
            b.w = ins
            if not acc:
                b.r = []
        return ins

    def finalize(self):
        for e in ENGINES:
            waited = {}
            for ins in self.q[e]:
                best = {}
                for d in ins.deps:
                    if d.dma is not None:
                        key = ("dma", d.dma[0])
                        if key not in best or best[key].dma[1] < d.dma[1]:
                            best[key] = d
                    else:
                        if d.eng == e and (e == "tensor" or not self.same_engine_sync):
                            continue
                        key = ("eng", d.eng)
                        if key not in best or best[key].idx < d.idx:
                            best[key] = d
                for key, d in best.items():
                    if d.dma is not None:
                        if waited.get(key, 0) >= d.dma[1]:
                            continue
                        waited[key] = d.dma[1]
                    else:
                        if waited.get(key, -1) >= d.idx:
                            continue
                        waited[key] = d.idx
                        d.mark = True
                    ins.waits.append(d)
        for e in ENGINES:
            c = 0
            for ins in self.q[e]:
                if ins.mark:
                    c += 1
                    ins.semval = c

    def stats(self):
        return {e: (len(self.q[e]), sum(1 for i in self.q[e] if i.mark), sum(len(i.waits) for i in self.q[e]))
                for e in ENGINES}

    def emit(self, block, esem, dsem):
        g = self

        def body(ename):
            def run(eng):
                for ins in g.q[ename]:
                    for d in ins.waits:
                        if d.dma is not None:
                            eng.wait_ge(dsem[d.dma[0]], d.dma[1])
                        else:
                            eng.wait_ge(esem[d.eng], d.semval)
                    r = ins.fn(eng)
                    if ins.dma is not None:
                        r.then_inc(dsem[ins.dma[0]], ins.inc)
                    elif ins.mark:
                        r.then_inc(esem[ename], 1)
            return run

        block.tensor(body("tensor"))
        block.vector(body("vector"))
        block.scalar(body("scalar"))
        block.gpsimd(body("gpsimd"))
        block.sync(body("sync"))


class Slot:
    __slots__ = ("i", "ap", "buf")

    def __init__(self, i, ap, buf):
        self.i = i
        self.ap = ap
        self.buf = buf


class Pool:
    def __init__(self, tens, n, name):
        self.slots = [Slot(i, tens[:, i, :], Buf(f"{name}{i}")) for i in range(n)]
        self.free = deque(range(n))
        self.name = name
        self.low = n

    def get(self):
        if not self.free:
            raise RuntimeError(f"pool {self.name} exhausted")
        s = self.slots[self.free.popleft()]
        self.low = min(self.low, len(self.free))
        return s

    def put(self, *ss):
        for s in ss:
            self.free.append(s.i)


def _kc(w):
    k, n = w.shape
    return np.ascontiguousarray(w.reshape(k // 128, 128, n).transpose(1, 0, 2))


def _mch(w, col0, ms):
    return np.concatenate([_kc(w[:, col0 + m * 128: col0 + (m + 1) * 128]).reshape(128, -1) for m in ms], axis=1)


def _units(inp):
    w_in = inp["w_in"][0]
    units = []
    units.append(_kc(w_in[:, 0:512]).reshape(128, -1))
    units.append(_kc(inp["pool_proj"][0]).reshape(128, -1))
    units.append(_mch(w_in, 2560, range(0, 4)))
    units.append(_mch(w_in, 2560, range(4, 8)))
    for a in range(4):
        units.append(np.concatenate([np.concatenate([_mch(w_in, 512, [c]), _mch(w_in, 1536, [c])], 1)
                                     for c in (2 * a, 2 * a + 1)], 1))
    for a in range(4):
        units.append(np.concatenate([np.concatenate([_mch(inp["lru_proj"][0], 0, [m]), _mch(w_in, 3584, [m])], 1)
                                     for m in (2 * a, 2 * a + 1)], 1))
    units.append(_mch(inp["w_out"][0], 0, range(0, 4)))
    units.append(_mch(inp["w_out"][0], 0, range(4, 8)))
    wf = inp["w_ffn_in"][0]
    for a in range(11):
        units.append(np.concatenate([np.concatenate([_mch(wf, 0, [f]), _mch(wf, DFF, [f])], 1)
                                     for f in (2 * a, 2 * a + 1)], 1))
    for m in range(8):
        units.append(_kc(inp["w_ffn_out"][0][:, m * 128:(m + 1) * 128]).reshape(128, -1))
    units.append(_kc(inp["w_ple_proj"][0]).reshape(128, -1))
    units.append(_mch(inp["w_ple_gate"][0], 0, range(0, 4)))
    units.append(_mch(inp["w_ple_gate"][0], 0, range(4, 8)))
    assert len(units) == NU
    return units


UNIT_COLS = [4096, 4096, 4096, 4096] + [4096] * 4 + [4096] * 4 + [4096, 4096] + [4096] * 11 + [2816] * 8 + [2048, 4096, 4096]


def _pmats(first):
    P = np.zeros((128, 16, 128), np.float32)
    for g, w in enumerate((2, 4, 8, 16)):
        for t in range(128):
            for s in range(max(0, t - w + 1), t + 1):
                P[s, g, t] = 1.0 / w
            P[t, g, t] -= 1.0
            for d in range(1, w):
                sp = t - d
                if sp < 0:
                    P[128 + sp, 4 + g, t] = 1.0 / w
                    P[16 + sp, 8 + g, t] = 1.0 / w
            if first:
                cnt = min(t + 1, w)
                for s in range(max(0, t - w + 1), t + 1):
                    P[s, 12 + g, t] = 1.0 / cnt
                P[t, 12 + g, t] -= 1.0
            else:
                P[:, 12 + g, t] = P[:, g, t]
    return P


def build(dbg=None, nblk=NBLK, do_pass1=True, same_engine_sync=True):
    nc = bass.Bass("TRN2", target_bir_lowering=False)
    xT_d = nc.dram_tensor("xT", [128, 8, HALO + TPC], F32, kind="ExternalInput").ap()
    pT_d = nc.dram_tensor("pT", [128, 2, TPC], F32, kind="ExternalInput").ap()
    wst_d = nc.dram_tensor("wstream", [NU, 128, UC], F32, kind="ExternalInput").ap()
    wres_d = nc.dram_tensor("wres", [128, 2560], F32, kind="ExternalInput").ap()
    vecs_d = nc.dram_tensor("vecs", [128, NV], F32, kind="ExternalInput").ap()
    pc_d = nc.dram_tensor("percore", [128, NPC], F32, kind="ExternalInput").ap()
    pm_d = nc.dram_tensor("pmats", [128, 16, 128], F32, kind="ExternalInput").ap()
    id_d = nc.dram_tensor("ident", [128, 128], F32, kind="ExternalInput").ap()
    cb_d = nc.dram_tensor("cbrow", [1, D], F32, kind="ExternalInput").ap()
    out_d = nc.dram_tensor("outT", [128, 8, TPC], F32, kind="ExternalOutput").ap()
    wbf_d = nc.dram_tensor("wbf", [NU, 128, UC], BF16)
    hls_d = nc.dram_tensor("hl_scr", [8, 128, TPC], F32)
    pcs_d = nc.dram_tensor("pc_scr", [8, 128, TPC], F32)
    agi_d = nc.dram_tensor("ag_in", [128, 16], F32)
    ago_d = nc.dram_tensor("ag_out", [NCORES * 128, 16], F32)
    dbg_d = None
    if dbg:
        dbg_d = nc.dram_tensor("dbg", [128, 16, WB], F32, kind="ExternalOutput").ap()
        dbgb_d = nc.dram_tensor("dbgb", [128, 16, WB], F32, kind="ExternalOutput").ap()

    G = Graph(same_engine_sync)
    NBF, NF32 = (44 if dbg else 46), 16
    with ExitStack() as st:
        def sb(name, shape, dt):
            return st.enter_context(nc.sbuf_tensor(name, shape, dt))

        hT_t = sb("hT", [128, 2, 8, WB], F32)
        bfp_t = sb("bfp", [128, NBF, WB], BF16)
        f32p_t = sb("f32p", [128, NF32, WB], F32)
        ring_t = sb("ring", [128, RING, UC], BF16)
        p_t = sb("pblk", [128, 2, NB], F32)
        wres_f = sb("wres_f", [128, 2560], F32)
        wres_b = sb("wres_b", [128, 2560], BF16)
        diag_b = sb("diag_b", [128, 32, 128], BF16)
        vecs = sb("vecs_sb", [128, NV], F32)
        xv = sb("xvecs", [128, NX], F32)
        pcv = sb("pcv", [128, NPC], F32)
        pm_f = sb("pm_f", [128, 16, 128], F32)
        pm_b = sb("pm_b", [128, 16, 128], BF16)
        id_f = sb("id_f", [128, 128], F32)
        ones_b = sb("ones_b", [128, 512], BF16)
        cb_f = sb("cb_f", [1, D], F32)
        cb_b = sb("cb_b", [1, D], BF16)
        hstate = sb("hstate", [128, 8], F32)
        pstate = sb("pstate", [128, 8], F32)
        actdummy = sb("actdummy", [128, 2], F32)
        agt = sb("agt", [128, 16], F32)
        agr = sb("agr", [128, NCORES, 16], F32)
        cmb = sb("cmb", [128, 4, 8], F32)
        psum = [st.enter_context(nc.psum_tensor(f"ps{i}", [128, NB], F32)) for i in range(8)]

        esem = {e: st.enter_context(nc.semaphore(f"s_{e}")) for e in ENGINES}
        dkeys = ["x0", "x1", "p", "cst0", "cst1", "cst2", "cst3", "cst4", "cst5", "agi", "cc", "ago", "dbg", "dbg0", "dbg1"] + [f"pq{u}" for u in range(NU)] + \
                [f"r{i}" for i in range(RING)] + [f"o{i}" for i in range(NF32)] + [f"l{i}" for i in range(NF32)]
        dsem = {k: st.enter_context(nc.semaphore(f"d_{k}")) for k in dkeys}
        block = st.enter_context(nc.Block())

        bfp = Pool(bfp_t, NBF, "bf")
        f32p = Pool(f32p_t, NF32, "f32")
        hbuf = [[Buf(f"h{i}_{k}") for k in range(8)] for i in range(2)]
        psb = [Buf(f"psb{i}") for i in range(8)]
        ps_next = [0]

        def getps():
            i = ps_next[0]
            ps_next[0] = (i + 1) % 8
            return psum[i], psb[i]

        ringb = [Buf(f"ring{i}") for i in range(RING)]
        B_const = Buf("const")
        B_pm = Buf("pm")
        B_pc = Buf("pc")
        B_cb = Buf("cb")
        B_wres = Buf("wres")
        B_diag = Buf("diag")
        B_vec = Buf("vec")
        B_xv = Buf("xv")
        B_hs = [Buf(f"hstate{c}") for c in range(8)]
        B_ps = [Buf(f"pstate{c}") for c in range(8)]
        B_hls = [[Buf(f"hls{c}_{b}") for b in range(NBLK)] for c in range(8)]
        B_pcs = [[Buf(f"pcs{c}_{b}") for b in range(NBLK)] for c in range(8)]
        B_dummy = Buf("actdummy")
        B_p = Buf("p")
        B_scr = [Buf(f"scr{u}") for u in range(NU)]
        out_instrs = []

        V = lambda c: vecs[:, c:c + 1]
        X = lambda c: xv[:, c:c + 1]

        def add(eng, fn, r=(), w=(), **kw):
            return G.add(eng, fn, reads=r, writes=w, **kw)

        add("sync", lambda e: e.dma_start(out=vecs[:], in_=vecs_d), w=[B_vec], dma_sem="cst0")
        add("sync", lambda e: e.dma_start(out=pcv[:], in_=pc_d), w=[B_pc], dma_sem="cst1")
        add("sync", lambda e: e.dma_start(out=pm_f[:], in_=pm_d), w=[B_pm], dma_sem="cst2")
        add("sync", lambda e: e.dma_start(out=id_f[:], in_=id_d), w=[B_diag], dma_sem="cst3")
        add("sync", lambda e: e.dma_start(out=cb_f[:], in_=cb_d), w=[B_cb], dma_sem="cst4")
        add("sync", lambda e: e.dma_start(out=wres_f[:], in_=wres_d), w=[B_wres], dma_sem="cst5")
        add("gpsimd", lambda e: e.memset(ones_b[:], 1.0), w=[B_const])
        add("gpsimd", lambda e: e.memset(hstate[:], 0.0), w=B_hs)
        add("gpsimd", lambda e: e.memset(pstate[:], 1.0), w=B_ps)
        add("gpsimd", lambda e: e.memset(actdummy[:], 1.0), w=[B_dummy])
        add("gpsimd", lambda e: e.tensor_copy(out=pm_b[:], in_=pm_f[:]), r=[B_pm], w=[B_pm])
        add("gpsimd", lambda e: e.tensor_copy(out=cb_b[:], in_=cb_f[:]), r=[B_cb], w=[B_cb])
        add("gpsimd", lambda e: e.tensor_copy(out=wres_b[:], in_=wres_f[:]), r=[B_wres], w=[B_wres])
        for j in range(4):
            for c in range(8):
                add("vector", lambda e, j=j, c=c: e.tensor_scalar(out=diag_b[:, j * 8 + c, :], in0=id_f[:], scalar1=V(V_CW + j * 8 + c),
                                                                    scalar2=None, op0=ALU.mult),
                    r=[B_diag, B_vec], w=[B_diag])
        for src, dst in ((V_BG0, X_HBG0), (V_BG1, X_HBG1), (V_BRG, X_HBRG), (V_BIG, X_HBIG)):
            add("vector", lambda e, s=src, d=dst: e.tensor_scalar(out=xv[:, d:d + 8], in0=vecs[:, s:s + 8], scalar1=0.5, scalar2=None, op0=ALU.mult),
                r=[B_vec], w=[B_xv])
        add("vector", lambda e: e.memset(xv[:, X_EPS:X_EPS + 1], EPS), w=[B_xv])
        add("scalar", lambda e: e.activation(out=xv[:, X_CH:X_CH + 8], in_=vecs[:, V_LAM:V_LAM + 8], func=AF.Exp, scale=-1.0), r=[B_vec], w=[B_xv])
        add("scalar", lambda e: e.activation(out=xv[:, X_CH:X_CH + 8], in_=xv[:, X_CH:X_CH + 8], func=AF.Ln, bias=1.0, scale=1.0), r=[B_xv], w=[B_xv])
        add("vector", lambda e: e.tensor_scalar(out=xv[:, X_CF:X_CF + 8], in0=xv[:, X_CH:X_CH + 8], scalar1=-8.0, scalar2=None, op0=ALU.mult), r=[B_xv], w=[B_xv])
        add("vector", lambda e: e.tensor_scalar(out=xv[:, X_CH:X_CH + 8], in0=xv[:, X_CH:X_CH + 8], scalar1=-4.0, scalar2=None, op0=ALU.mult), r=[B_xv], w=[B_xv])

        cast_order = [4, 5, 6, 7] + [0, 1, 2, 3] + list(range(8, NU))
        cast_pos = [0]

        def issue_casts(n):
            for _ in range(n):
                if cast_pos[0] >= NU:
                    return
                u = cast_order[cast_pos[0]]
                cast_pos[0] += 1
                ncol = UNIT_COLS[u]
                bsz = 2048 if ncol % 2048 == 0 else 1408
                add("gpsimd", lambda e, u=u, ncol=ncol, bsz=bsz: e.dma_start(
                    out=wbf_d[u, :, 0:ncol].rearrange("p (a b) -> p a b", b=bsz),
                    in_=wst_d[u, :, 0:ncol].rearrange("p (a b) -> p a b", b=bsz)),
                    w=[B_scr[u]], dma_sem=f"pq{u}")

        issue_casts(4)

        class Ring:
            def __init__(self):
                self.seq = []
                self.loaded = 0
                self.cur = 0
                self.open = {}
                self.released = set()

            def _load(self, n):
                u = self.seq[n]
                ri = n % RING
                ncol = UNIT_COLS[u]
                add("sync", lambda e, u=u, ri=ri, ncol=ncol: e.dma_start(out=ring_t[:, ri, 0:ncol], in_=wbf_d[u, :, 0:ncol]),
                    r=[B_scr[u]], w=[ringb[ri]], dma_sem=f"r{ri}")

            def _prefetch(self):
                while self.loaded < len(self.seq) and (self.loaded < RING or (self.loaded - RING) in self.released):
                    self._load(self.loaded)
                    self.loaded += 1

            def next(self, u, keep=()):
                for n_, u_ in list(self.open.items()):
                    if u_ not in keep:
                        self.released.add(n_)
                        del self.open[n_]
                n = self.cur
                assert self.seq[n] == u, (n, self.seq[n], u)
                self.open[n] = u
                self._prefetch()
                assert self.loaded > n
                self.cur += 1
                ri = n % RING
                return (lambda c0, w, ri=ri: ring_t[:, ri, c0:c0 + w]), ringb[ri]

        ring = Ring()

        def mm(ps, pb, out_ap, lhsT, rhs, reads, first, start, stop):
            add("tensor", lambda e: e.matmul(out_ap, lhsT=lhsT, rhs=rhs, start=start, stop=stop),
                r=reads, w=[pb], acc=not first)

        def norm(hi, gcol, with_halo):
            c0 = 0 if with_halo else HALO
            add("scalar", lambda e: e.activation(out=actdummy[:, 1:2], in_=actdummy[:, 0:1], func=AF.Ln), r=[B_dummy], w=[B_dummy])
            sq = [bfp.get() for _ in range(8)]
            for k in range(8):
                add("scalar", lambda e, k=k, s=sq[k]: e.activation(out=s.ap[:, c0:WB], in_=hT_t[:, hi, k, c0:WB], func=AF.Square),
                    r=[hbuf[hi][k]], w=[sq[k].buf])
            psm, pbm = getps()
            for k in range(8):
                mm(psm, pbm, psm[:, :], ones_b[:, 0:128], sq[k].ap[:, HALO:WB], [sq[k].buf, B_const], k == 0, k == 0, k == 7)
            if with_halo:
                psh, pbh = getps()
                for k in range(8):
                    mm(psh, pbh, psh[:, 0:HALO], ones_b[:, 0:128], sq[k].ap[:, 0:HALO], [sq[k].buf, B_const], k == 0, k == 0, k == 7)
            bfp.put(*sq)
            add("scalar", lambda e: e.activation(out=psm[:, :], in_=psm[:, :], func=AF.Ln, scale=1.0 / D, bias=X(X_EPS)), r=[pbm, B_xv], w=[pbm])
            add("scalar", lambda e: e.activation(out=psm[:, :], in_=psm[:, :], func=AF.Exp, scale=-0.5), r=[pbm], w=[pbm])
            if with_halo:
                add("scalar", lambda e: e.activation(out=psh[:, 0:HALO], in_=psh[:, 0:HALO], func=AF.Ln, scale=1.0 / D, bias=X(X_EPS)), r=[pbh, B_xv], w=[pbh])
                add("scalar", lambda e: e.activation(out=psh[:, 0:HALO], in_=psh[:, 0:HALO], func=AF.Exp, scale=-0.5), r=[pbh], w=[pbh])
            return psm, pbm, (psh, pbh) if with_halo else None

        def scale_u(hi, gcol, psm, pbm, halo):
            us = [bfp.get() for _ in range(8)]
            for k in range(8):
                add("vector", lambda e, k=k, s=us[k]: e.scalar_tensor_tensor(out=s.ap[:, HALO:WB], in0=hT_t[:, hi, k, HALO:WB], scalar=V(gcol + k),
                                                                          in1=psm[:, :], op0=ALU.mult, op1=ALU.mult),
                    r=[hbuf[hi][k], pbm, B_vec], w=[us[k].buf])
                if halo is not None:
                    psh, pbh = halo
                    add("vector", lambda e, k=k, s=us[k], psh=psh: e.scalar_tensor_tensor(out=s.ap[:, 0:HALO], in0=hT_t[:, hi, k, 0:HALO], scalar=V(gcol + k),
                                                                                       in1=psh[:, 0:HALO], op0=ALU.mult, op1=ALU.mult),
                        r=[hbuf[hi][k], pbh, B_vec], w=[us[k].buf])
            return us

        def tap(slot_i, ap, buf, width):
            if dbg_d is None:
                return
            add("sync", lambda e: e.dma_start(out=dbg_d[:, slot_i, 0:width], in_=ap), r=[buf], w=[Buf("dbgo")], dma_sem="dbg")

        dbg_stage = [None]
        dbg_n = [0]

        def tapb(slot_i, ap, buf, width, parts=128):
            if dbg_d is None:
                return
            if dbg_stage[0] is None:
                dbg_stage[0] = (sb("dbgstage", [128, 2, WB], F32), [Buf("dbgs0"), Buf("dbgs1")])
            t, bs = dbg_stage[0]
            i = dbg_n[0] % 2
            dbg_n[0] += 1
            add("gpsimd", lambda e: e.tensor_copy(out=t[0:parts, i, 0:width], in_=ap), r=[buf], w=[bs[i]])
            add("sync", lambda e: e.dma_start(out=dbgb_d[0:parts, slot_i, 0:width], in_=t[0:parts, i, 0:width]), r=[bs[i]], w=[Buf("dbgo")], dma_sem=f"dbg{i}")

        def lru_front(a, us, wf, wbuf, pass1, blk):
            cs = (2 * a, 2 * a + 1)
            st_ = {"cs": cs, "xcb": {}, "ge": {}, "hl": {}, "pc": {}}
            if not pass1:
                for c in cs:
                    hl = f32p.get()
                    pc = f32p.get()
                    add("sync", lambda e, c=c, hl=hl: e.dma_start(out=hl.ap[:, 0:NB], in_=hls_d[c, :, blk * NB:(blk + 1) * NB]),
                        r=[B_hls[c][blk]], w=[hl.buf], dma_sem=f"l{hl.i}")
                    add("sync", lambda e, c=c, pc=pc: e.dma_start(out=pc.ap[:, 0:NB], in_=pcs_d[c, :, blk * NB:(blk + 1) * NB]),
                        r=[B_pcs[c][blk]], w=[pc.buf], dma_sem=f"l{pc.i}")
                    st_["hl"][c] = hl
                    st_["pc"][c] = pc
                for c in cs:
                    i = c % 2
                    p_, pb_ = getps()
                    for k in range(8):
                        mm(p_, pb_, p_[:, :], wf(i * 2048 + 1024 + k * 128, 128), us[k].ap[:, HALO:WB], [wbuf, us[k].buf], k == 0, k == 0, k == 7)
                    ge = f32p.get()
                    add("scalar", lambda e, ge=ge, p_=p_: e.activation(out=ge.ap[:, 0:NB], in_=p_[:, :], func=AF.Gelu_apprx_tanh), r=[pb_], w=[ge.buf])
                    st_["ge"][c] = ge
                return st_
            zl = {}
            for c in cs:
                i = c % 2
                pz, pzb = getps()
                for k in range(8):
                    mm(pz, pzb, pz[:, :], wf(i * 2048 + k * 128, 128), us[k].ap[:, HALO:WB], [wbuf, us[k].buf], k == 0, k == 0, k == 7)
                ph, phb = getps()
                for k in range(8):
                    mm(ph, phb, ph[:, 0:HALO], wf(i * 2048 + k * 128, 128), us[k].ap[:, 0:HALO], [wbuf, us[k].buf], k == 0, k == 0, k == 7)
                z = bfp.get()
                add("scalar", lambda e, z=z, pz=pz: e.activation(out=z.ap[:, HALO:WB], in_=pz[:, :], func=AF.Copy), r=[pzb], w=[z.buf])
                add("vector", lambda e, z=z, ph=ph: e.tensor_copy(out=z.ap[:, 0:HALO], in_=ph[:, 0:HALO]), r=[phb], w=[z.buf])
                zl[c] = z
            for c in cs:
                px, pxb = getps()
                for j in range(4):
                    mm(px, pxb, px[:, :], diag_b[:, j * 8 + c, :], zl[c].ap[:, HALO - 3 + j: HALO - 3 + j + NB], [zl[c].buf, B_diag], j == 0, j == 0, False)
                mm(px, pxb, px[:, :], cb_b[0:1, c * 128:(c + 1) * 128], ones_b[0:1, 0:NB], [B_const, B_cb], False, False, True)
                bfp.put(zl[c])
                xb = bfp.get()
                add("vector", lambda e, xb=xb, px=px: e.tensor_copy(out=xb.ap[:, 0:NB], in_=px[:, :]), r=[pxb], w=[xb.buf])
                st_["xcb"][c] = xb
            return st_

        def lru_back(st_, blk, first_block, pass1, hg_out):
            cs = st_["cs"]
            if not pass1:
                for c in cs:
                    hl, pc, ge = st_["hl"][c], st_["pc"][c], st_["ge"][c]
                    add("vector", lambda e, c=c, hl=hl, pc=pc: e.scalar_tensor_tensor(out=hl.ap[:, 0:NB], in0=pc.ap[:, 0:NB], scalar=hstate[:, c:c + 1],
                                                                                    in1=hl.ap[:, 0:NB], op0=ALU.mult, op1=ALU.add),
                        r=[hl.buf, pc.buf, B_hs[c]], w=[hl.buf])
                    if dbg == "det" and blk == 0 and c == 0:
                        tap(0, hl.ap[:, 0:NB], hl.buf, NB)
                    hg = bfp.get()
                    add("gpsimd", lambda e, hg=hg, ge=ge, hl=hl: e.tensor_tensor(out=hg.ap[:, 0:NB], in0=ge.ap[:, 0:NB], in1=hl.ap[:, 0:NB], op=ALU.mult),
                        r=[ge.buf, hl.buf], w=[hg.buf])
                    hg_out[c] = hg
                    f32p.put(ge, hl, pc)
                return
            xcb = st_["xcb"]
            A, I, M = {}, {}, {}
            for c in cs:
                pr, prb = getps()
                mm(pr, prb, pr[:, :], wres_b[:, 512 + c * 128: 512 + (c + 1) * 128], xcb[c].ap[:, 0:NB], [xcb[c].buf, B_wres], True, True, True)
                pi, pib = getps()
                mm(pi, pib, pi[:, :], wres_b[:, 1536 + c * 128: 1536 + (c + 1) * 128], xcb[c].ap[:, 0:NB], [xcb[c].buf, B_wres], True, True, True)
                A[c] = f32p.get()
                I[c] = f32p.get()
                add("scalar", lambda e, c=c, s=A[c], pr=pr: e.activation(out=s.ap[:, 0:NB], in_=pr[:, :], func=AF.Tanh, scale=0.5, bias=X(X_HBRG + c)),
                    r=[prb, B_xv], w=[A[c].buf])
                add("scalar", lambda e, c=c, s=I[c], pi=pi: e.activation(out=s.ap[:, 0:NB], in_=pi[:, :], func=AF.Tanh, scale=0.5, bias=X(X_HBIG + c)),
                    r=[pib, B_xv], w=[I[c].buf])
            for c in cs:
                add("scalar", lambda e, c=c, s=A[c]: e.activation(out=s.ap[:, 0:NB], in_=s.ap[:, 0:NB], func=AF.Exp, scale=X(X_CH + c), bias=X(X_CH + c)),
                    r=[A[c].buf, B_xv], w=[A[c].buf])
                M[c] = f32p.get()
                add("gpsimd", lambda e, m=M[c], s=A[c]: e.tensor_tensor(out=m.ap[:, 0:NB], in0=s.ap[:, 0:NB], in1=s.ap[:, 0:NB], op=ALU.mult),
                    r=[A[c].buf], w=[M[c].buf])
            issue_casts(1)
            for c in cs:
                add("scalar", lambda e, m=M[c]: e.activation(out=m.ap[:, 0:NB], in_=m.ap[:, 0:NB], func=AF.Sqrt, scale=-1.0, bias=1.0),
                    r=[M[c].buf], w=[M[c].buf])
                if first_block:
                    add("vector", lambda e, m=M[c]: e.scalar_tensor_tensor(out=m.ap[:, 0:1], in0=m.ap[:, 0:1], scalar=pcv[:, PC_OMF:PC_OMF + 1],
                                                                          in1=pcv[:, PC_FLAG:PC_FLAG + 1], op0=ALU.mult, op1=ALU.add),
                        r=[M[c].buf, B_pc], w=[M[c].buf])
            for c in cs:
                add("vector", lambda e, s=I[c], xb=xcb[c]: e.scalar_tensor_tensor(out=s.ap[:, 0:NB], in0=s.ap[:, 0:NB], scalar=1.0, in1=xb.ap[:, 0:NB], op0=ALU.add, op1=ALU.mult),
                    r=[I[c].buf, xcb[c].buf], w=[I[c].buf])
                bfp.put(xcb[c])
                add("vector", lambda e, s=I[c], m=M[c]: e.scalar_tensor_tensor(out=s.ap[:, 0:NB], in0=s.ap[:, 0:NB], scalar=0.5, in1=m.ap[:, 0:NB], op0=ALU.mult, op1=ALU.mult),
                    r=[I[c].buf, M[c].buf], w=[I[c].buf])
                add("vector", lambda e, c=c, s=I[c], aa=A[c], m=M[c]: e.tensor_tensor_scan(out=m.ap[:, 0:NB], data0=aa.ap[:, 0:NB], data1=s.ap[:, 0:NB],
                                                                                      initial=hstate[:, c:c + 1], op0=ALU.mult, op1=ALU.add),
                    r=[A[c].buf, I[c].buf, B_hs[c]], w=[M[c].buf])
                add("vector", lambda e, c=c, m=M[c]: e.tensor_copy(out=hstate[:, c:c + 1], in_=m.ap[:, NB - 1:NB]), r=[M[c].buf], w=[B_hs[c]])
                add("vector", lambda e, c=c, s=I[c], aa=A[c]: e.tensor_tensor_scan(out=s.ap[:, 0:NB], data0=aa.ap[:, 0:NB], data1=aa.ap[:, 0:NB],
                                                                                initial=pstate[:, c:c + 1], op0=ALU.mult, op1=ALU.min),
                    r=[A[c].buf, B_ps[c]], w=[I[c].buf])
                add("vector", lambda e, c=c, s=I[c]: e.tensor_copy(out=pstate[:, c:c + 1], in_=s.ap[:, NB - 1:NB]), r=[I[c].buf], w=[B_ps[c]])
                add("sync", lambda e, c=c, m=M[c]: e.dma_start(out=hls_d[c, :, blk * NB:(blk + 1) * NB], in_=m.ap[:, 0:NB]),
                    r=[M[c].buf], w=[B_hls[c][blk]], dma_sem=f"o{M[c].i}")
                add("sync", lambda e, c=c, s=I[c]: e.dma_start(out=pcs_d[c, :, blk * NB:(blk + 1) * NB], in_=s.ap[:, 0:NB]),
                    r=[I[c].buf], w=[B_pcs[c][blk]], dma_sem=f"o{I[c].i}")
                f32p.put(A[c], I[c], M[c])

        def lru_all(us, blk, pass1, hg_out, unit_of, mid_hook=None):
            sts = {}
            for a in range(5):
                if a < 4:
                    wf, wbuf = unit_of(a)
                    sts[a] = lru_front(a, us, wf, wbuf, pass1, blk)
                if a >= 1:
                    lru_back(sts.pop(a - 1), blk, blk == 0, pass1, hg_out)
                if a == 2 and mid_hook is not None:
                    mid_hook()

        def load_x(hi, blk):
            add("sync", lambda e: e.dma_start(out=hT_t[:, hi, :, :], in_=xT_d[:, :, blk * NB: blk * NB + WB]),
                w=hbuf[hi], dma_sem=f"x{hi}")

        if do_pass1:
            pinned = []
            for a in range(4):
                u = 4 + a
                add("sync", lambda e, u=u, a=a: e.dma_start(out=ring_t[:, a, :], in_=wbf_d[u, :, :]), r=[B_scr[u]], w=[ringb[a]], dma_sem=f"r{a}")
                pinned.append((lambda c0, w, a=a: ring_t[:, a, c0:c0 + w], ringb[a]))
            load_x(0, 0)
            psm, pbm, halo = norm(0, V_G1, True)
            us_next = [scale_u(0, V_G1, psm, pbm, halo)]
            for blk in range(nblk):
                hi = blk % 2
                us = us_next[0]

                def hook(blk=blk, hi=hi):
                    if blk + 1 < nblk:
                        load_x(1 - hi, blk + 1)
                        psm_, pbm_, halo_ = norm(1 - hi, V_G1, True)
                        us_next[0] = scale_u(1 - hi, V_G1, psm_, pbm_, halo_)

                lru_all(us, blk, True, None, lambda a: pinned[a], mid_hook=hook)
                bfp.put(*us)
            issue_casts(NU)
            B_agt = Buf("agt")
            add("vector", lambda e: e.tensor_copy(out=agt[:, 0:8], in_=pstate[:, :]), r=B_ps, w=[B_agt])
            add("vector", lambda e: e.tensor_copy(out=agt[:, 8:16], in_=hstate[:, :]), r=B_hs, w=[B_agt])
            B_agi, B_ago, B_agr = Buf("agi"), Buf("ago"), Buf("agr")
            add("gpsimd", lambda e: e.dma_start(out=agi_d[:, :], in_=agt[:]), r=[B_agt], w=[B_agi], dma_sem="agi")
            add("gpsimd", lambda e: e.collective_compute("AllGather", ALU.bypass, replica_groups=[list(range(NCORES))],
                                                          ins=[agi_d.ap().opt()], outs=[ago_d.ap().opt()]),
                r=[B_agi], w=[B_ago], dma_sem="cc", inc=1)
            add("gpsimd", lambda e: e.dma_start(out=agr[:], in_=ago_d.ap().rearrange("(r p) f -> p r f", p=128)), r=[B_ago], w=[B_agr], dma_sem="ago")
            add("vector", lambda e: e.memset(hstate[:], 0.0), r=[B_agt], w=B_hs)
            B_cmb = Buf("cmb")
            for r_ in range(NCORES):
                add("vector", lambda e, r_=r_: e.tensor_scalar(out=cmb[:, 0, :], in0=agr[:, r_, 0:8], scalar1=pcv[:, PC_M + r_:PC_M + r_ + 1],
                                                                 scalar2=pcv[:, PC_OMM + r_:PC_OMM + r_ + 1], op0=ALU.mult, op1=ALU.add),
                    r=[B_agr, B_pc], w=[B_cmb])
                add("vector", lambda e, r_=r_: e.tensor_scalar(out=cmb[:, 1, :], in0=agr[:, r_, 8:16], scalar1=pcv[:, PC_M + r_:PC_M + r_ + 1],
                                                                 scalar2=None, op0=ALU.mult),
                    r=[B_agr, B_pc], w=[B_cmb])
                add("vector", lambda e: e.tensor_tensor(out=cmb[:, 2, :], in0=cmb[:, 0, :], in1=hstate[:, :], op=ALU.mult), r=[B_cmb] + B_hs, w=[B_cmb])
                add("vector", lambda e: e.tensor_tensor(out=hstate[:, :], in0=cmb[:, 2, :], in1=cmb[:, 1, :], op=ALU.add), r=[B_cmb], w=B_hs)

        issue_casts(NU)
        ring.seq = [u for _ in range(nblk) for u in range(NU)]
        load_x(0, 0)
        psm, pbm, halo = norm(0, V_G1, True)
        us_next = scale_u(0, V_G1, psm, pbm, halo)
        for blk in range(nblk):
            hi = blk % 2
            if blk + 1 < nblk:
                load_x(1 - hi, blk + 1)
            add("sync", lambda e, blk=blk: e.dma_start(out=p_t[:, :, :], in_=pT_d[:, :, blk * NB:(blk + 1) * NB]), w=[B_p], dma_sem="p")
            Hm = [hT_t[:, hi, k, HALO:WB] for k in range(8)]
            us = us_next
            if dbg == "det" and blk == 0:
                tapb(0, us[0].ap[:, :], us[0].buf, WB)
                tapb(1, us[7].ap[:, :], us[7].buf, WB)
            wf, wbuf = ring.next(0)
            zp = []
            for ti in range(5):
                lo, wdt = (0, HALO) if ti == 0 else (HALO + (ti - 1) * 128, 128)
                pz, pzb = getps()
                for k in range(8):
                    mm(pz, pzb, pz[0:wdt, :], us[k].ap[:, lo:lo + wdt], wf(k * 512, 512), [wbuf, us[k].buf], k == 0, k == 0, k == 7)
                z = bfp.get()
                add("vector", lambda e, z=z, pz=pz, wdt=wdt: e.tensor_copy(out=z.ap[0:wdt, 0:512], in_=pz[0:wdt, :]), r=[pzb], w=[z.buf])
                zp.append(z)
                if dbg == "det" and blk == 0 and ti in (0, 1, 2):
                    tapb(2 + ti, z.ap[0:wdt, 0:512], z.buf, 512, parts=wdt)
            mixed = []
            for g in range(4):
                pp_, ppb = getps()
                for ti in range(1, 5):
                    dsel = (12 + g) if (blk == 0 and ti == 1) else g
                    osl = pp_[:, (ti - 1) * 128: ti * 128]
                    mm(pp_, ppb, osl, zp[ti].ap[:, g * 128:(g + 1) * 128], pm_b[:, dsel, :], [zp[ti].buf, B_pm], ti == 1, True, False)
                    if ti == 1:
                        mm(pp_, ppb, osl, zp[0].ap[0:HALO, g * 128:(g + 1) * 128], pm_b[0:HALO, 8 + g, :], [zp[0].buf, B_pm], False, False, True)
                    else:
                        mm(pp_, ppb, osl, zp[ti - 1].ap[:, g * 128:(g + 1) * 128], pm_b[:, 4 + g, :], [zp[ti - 1].buf, B_pm], False, False, True)
                pl = bfp.get()
                add("scalar", lambda e, pl=pl, pp_=pp_: e.activation(out=pl.ap[:, 0:NB], in_=pp_[:, :], func=AF.Copy), r=[ppb], w=[pl.buf])
                if dbg == "det" and blk == 0 and g in (0, 3):
                    tapb(5 + (g // 3), pl.ap[:, 0:NB], pl.buf, NB)
                pq, pqb = getps()
                mm(pq, pqb, pq[:, :], wres_b[:, g * 128:(g + 1) * 128], pl.ap[:, 0:NB], [pl.buf, B_wres], True, True, True)
                bfp.put(pl)
                mx = bfp.get()
                add("scalar", lambda e, g=g, mx=mx, pq=pq: e.activation(out=mx.ap[:, 0:NB], in_=pq[:, :], func=AF.Identity, scale=V(V_PSC + g)),
                    r=[pqb, B_vec], w=[mx.buf])
                mixed.append(mx)
                if dbg == "det" and blk == 0 and g == 0:
                    tapb(7, mx.ap[:, 0:NB], mx.buf, NB)
            bfp.put(*zp)
            wfp, wbp = ring.next(1)
            m0 = []
            for m in range(8):
                if m % 4 == 0:
                    wfg, wbg = ring.next(2 + m // 4, keep=(1,))
                py, pyb = getps()
                for g in range(4):
                    mm(py, pyb, py[:, :], wfp(g * 1024 + m * 128, 128), mixed[g].ap[:, 0:NB], [wbp, mixed[g].buf], g == 0, g == 0, g == 3)
                pg_, pgb = getps()
                for k in range(8):
                    mm(pg_, pgb, pg_[:, :], wfg(((m % 4) * 8 + k) * 128, 128), us[k].ap[:, HALO:WB], [wbg, us[k].buf], k == 0, k == 0, k == 7)
                th = f32p.get()
                add("scalar", lambda e, m=m, th=th, pg_=pg_: e.activation(out=th.ap[:, 0:NB], in_=pg_[:, :], func=AF.Tanh, scale=0.5, bias=X(X_HBG0 + m)),
                    r=[pgb, B_xv], w=[th.buf])
                mo = bfp.get()
                add("vector", lambda e, th=th, py=py, mo=mo: e.scalar_tensor_tensor(out=mo.ap[:, 0:NB], in0=th.ap[:, 0:NB], scalar=1.0, in1=py[:, :], op0=ALU.add, op1=ALU.mult),
                    r=[th.buf, pyb], w=[mo.buf])
                f32p.put(th)
                m0.append(mo)
                if dbg == "det" and blk == 0 and m == 0:
                    tapb(8, mo.ap[:, 0:NB], mo.buf, NB)
            bfp.put(*mixed)
            hg = {}
            lru_all(us, blk, False, hg, lambda a: ring.next(4 + a))
            if dbg == "det" and blk == 0:
                tapb(9, hg[0].ap[:, 0:NB], hg[0].buf, NB)
                tapb(10, hg[5].ap[:, 0:NB], hg[5].buf, NB)
            mg = []
            for m in range(8):
                if m % 2 == 0:
                    wf, wbuf = ring.next(8 + m // 2)
                i = m % 2
                py, pyb = getps()
                for c in range(8):
                    mm(py, pyb, py[:, :], wf(i * 2048 + c * 128, 128), hg[c].ap[:, 0:NB], [wbuf, hg[c].buf], c == 0, c == 0, c == 7)
                pg_, pgb = getps()
                for k in range(8):
                    mm(pg_, pgb, pg_[:, :], wf(i * 2048 + 1024 + k * 128, 128), us[k].ap[:, HALO:WB], [wbuf, us[k].buf], k == 0, k == 0, k == 7)
                th = f32p.get()
                add("scalar", lambda e, m=m, th=th, pg_=pg_: e.activation(out=th.ap[:, 0:NB], in_=pg_[:, :], func=AF.Tanh, scale=0.5, bias=X(X_HBG1 + m)),
                    r=[pgb, B_xv], w=[th.buf])
                add("vector", lambda e, th=th, py=py: e.scalar_tensor_tensor(out=th.ap[:, 0:NB], in0=th.ap[:, 0:NB], scalar=1.0, in1=py[:, :], op0=ALU.add, op1=ALU.mult),
                    r=[th.buf, pyb], w=[th.buf])
                mgs = bfp.get()
                add("gpsimd", lambda e, th=th, mo=m0[m], mgs=mgs: e.tensor_tensor(out=mgs.ap[:, 0:NB], in0=th.ap[:, 0:NB], in1=mo.ap[:, 0:NB], op=ALU.add),
                    r=[th.buf, m0[m].buf], w=[mgs.buf])
                f32p.put(th)
                mg.append(mgs)
                if dbg == "det" and blk == 0 and m == 0:
                    tapb(11, mgs.ap[:, 0:NB], mgs.buf, NB)
            bfp.put(*m0)
            bfp.put(*[hg[c] for c in range(8)])
            bfp.put(*us)
            wo_units = [ring.next(12), ring.next(13, keep=(12,))]
            pos = [getps() for _ in range(8)]
            for k in range(8):
                for m in range(8):
                    wf, wbuf = wo_units[m // 4]
                    po, pob = pos[m]
                    mm(po, pob, po[:, :], wf(((m % 4) * 8 + k) * 128, 128), mg[k].ap[:, 0:NB], [wbuf, mg[k].buf], k == 0, k == 0, k == 7)
            for m in range(8):
                po, pob = pos[m]
                add("vector", lambda e, hm=Hm[m], po=po: e.scalar_tensor_tensor(out=hm, in0=po[:, :], scalar=0.5, in1=hm, op0=ALU.mult, op1=ALU.add),
                    r=[pob, hbuf[hi][m]], w=[hbuf[hi][m]])
            bfp.put(*mg)
            if dbg == "mix":
                for k in range(8):
                    tap(k, hT_t[:, hi, k, :], hbuf[hi][k], WB)
            psm, pbm, _ = norm(hi, V_G2, False)
            vs = scale_u(hi, V_G2, psm, pbm, None)
            ff = []
            for f in range(NF):
                if f % 2 == 0:
                    wf, wbuf = ring.next(14 + f // 2)
                i = f % 2
                pg_, pgb = getps()
                for k in range(8):
                    mm(pg_, pgb, pg_[:, :], wf(i * 2048 + k * 128, 128), vs[k].ap[:, HALO:WB], [wbuf, vs[k].buf], k == 0, k == 0, k == 7)
                pu, pub = getps()
                for k in range(8):
                    mm(pu, pub, pu[:, :], wf(i * 2048 + 1024 + k * 128, 128), vs[k].ap[:, HALO:WB], [wbuf, vs[k].buf], k == 0, k == 0, k == 7)
                th = f32p.get()
                add("scalar", lambda e, th=th, pg_=pg_: e.activation(out=th.ap[:, 0:NB], in_=pg_[:, :], func=AF.Tanh, scale=0.5), r=[pgb], w=[th.buf])
                add("vector", lambda e, th=th, pg_=pg_: e.scalar_tensor_tensor(out=th.ap[:, 0:NB], in0=th.ap[:, 0:NB], scalar=1.0, in1=pg_[:, :], op0=ALU.add, op1=ALU.mult),
                    r=[th.buf, pgb], w=[th.buf])
                fs = bfp.get()
                add("vector", lambda e, th=th, pu=pu, fs=fs: e.tensor_tensor(out=fs.ap[:, 0:NB], in0=th.ap[:, 0:NB], in1=pu[:, :], op=ALU.mult),
                    r=[th.buf, pub], w=[fs.buf])
                f32p.put(th)
                ff.append(fs)
            bfp.put(*vs)
            if blk + 1 < nblk:
                psm, pbm, halo = norm(1 - hi, V_G1, True)
                us_next = scale_u(1 - hi, V_G1, psm, pbm, halo)
            for m in range(8):
                wf, wbuf = ring.next(25 + m)
                po, pob = getps()
                for f in range(NF):
                    mm(po, pob, po[:, :], wf(f * 128, 128), ff[f].ap[:, 0:NB], [wbuf, ff[f].buf], f == 0, f == 0, f == NF - 1)
                add("vector", lambda e, hm=Hm[m], po=po: e.scalar_tensor_tensor(out=hm, in0=po[:, :], scalar=0.5, in1=hm, op0=ALU.mult, op1=ALU.add),
                    r=[pob, hbuf[hi][m]], w=[hbuf[hi][m]])
            bfp.put(*ff)
            if dbg == "ffn":
                for k in range(8):
                    tap(k, hT_t[:, hi, k, :], hbuf[hi][k], WB)
            pb16 = [bfp.get() for _ in range(2)]
            for j in range(2):
                add("gpsimd", lambda e, j=j, s=pb16[j]: e.tensor_copy(out=s.ap[:, 0:NB], in_=p_t[:, j, :]), r=[B_p], w=[pb16[j].buf])
            psm, pbm, _ = norm(hi, V_G3, False)
            ws = scale_u(hi, V_G3, psm, pbm, None)
            wfe, wbe = ring.next(33)
            for m in range(8):
                if m % 4 == 0:
                    wf, wbuf = ring.next(34 + m // 4, keep=(33,))
                pe, peb = getps()
                for j in range(2):
                    mm(pe, peb, pe[:, :], wfe(j * 1024 + m * 128, 128), pb16[j].ap[:, 0:NB], [wbe, pb16[j].buf], j == 0, j == 0, j == 1)
                pg_, pgb = getps()
                for k in range(8):
                    mm(pg_, pgb, pg_[:, :], wf(((m % 4) * 8 + k) * 128, 128), ws[k].ap[:, HALO:WB], [wbuf, ws[k].buf], k == 0, k == 0, k == 7)
                th = f32p.get()
                add("scalar", lambda e, th=th, pg_=pg_: e.activation(out=th.ap[:, 0:NB], in_=pg_[:, :], func=AF.Tanh, scale=0.5), r=[pgb], w=[th.buf])
                add("vector", lambda e, th=th, pe=pe: e.scalar_tensor_tensor(out=th.ap[:, 0:NB], in0=th.ap[:, 0:NB], scalar=1.0, in1=pe[:, :], op0=ALU.add, op1=ALU.mult),
                    r=[th.buf, peb], w=[th.buf])
                add("vector", lambda e, hm=Hm[m], th=th: e.scalar_tensor_tensor(out=hm, in0=th.ap[:, 0:NB], scalar=0.5, in1=hm, op0=ALU.mult, op1=ALU.add),
                    r=[th.buf, hbuf[hi][m]], w=[hbuf[hi][m]])
                f32p.put(th)
            bfp.put(*ws)
            bfp.put(*pb16)
            psm, pbm, _ = norm(hi, V_GF, False)
            for m in range(8):
                o = f32p.get()
                add("vector", lambda e, m=m, o=o, hm=Hm[m], psm=psm: e.scalar_tensor_tensor(out=o.ap[:, 0:NB], in0=hm, scalar=V(V_GF + m), in1=psm[:, :], op0=ALU.mult, op1=ALU.mult),
                    r=[hbuf[hi][m], pbm, B_vec], w=[o.buf])
                ins = add("sync", lambda e, m=m, o=o, blk=blk: e.dma_start(out=out_d[:, m, blk * NB:(blk + 1) * NB], in_=o.ap[:, 0:NB]),
                          r=[o.buf], w=[Buf("outd")], dma_sem=f"o{o.i}")
                out_instrs.append(ins)
                f32p.put(o)

        last = add("sync", lambda e: e.nop())
        last.deps.extend(out_instrs)
        last.deps.extend([i for i in G.q["sync"] if i.dma is not None and i.dma[0].startswith("dbg")])
        G.finalize()
        build.stats = (G.stats(), bfp.low, f32p.low)
        G.emit(block, esem, dsem)
    return nc


def _host_inputs(inp):
    x = np.asarray(inp["x"], np.float32)
    p = np.asarray(inp["p"], np.float32)[0]
    I = {k: np.asarray(v, np.float32) for k, v in inp.items()}
    units = _units(I)
    wstream = np.zeros((NU, 128, UC), np.float32)
    for u, a in enumerate(units):
        assert a.shape[1] == UNIT_COLS[u], (u, a.shape)
        wstream[u, :, :a.shape[1]] = a
    col = lambda v: np.ascontiguousarray(v.reshape(-1, 128).T)
    vecs = np.zeros((128, NV), np.float32)
    vecs[:, V_G1:V_G1 + 8] = col(I["norm1_g"][0])
    vecs[:, V_G2:V_G2 + 8] = col(I["norm2_g"][0])
    vecs[:, V_G3:V_G3 + 8] = col(I["ple_norm_g"][0])
    vecs[:, V_GF:V_GF + 8] = col(I["final_g"])
    vecs[:, V_BG0:V_BG0 + 8] = col(I["b_gate"][0, 0])
    vecs[:, V_BG1:V_BG1 + 8] = col(I["b_gate"][0, 1])
    vecs[:, V_PSC:V_PSC + 4] = col(I["pool_scale"][0])
    for j in range(4):
        vecs[:, V_CW + j * 8: V_CW + j * 8 + 8] = col(I["conv_w"][0, j])
    vecs[:, V_BRG:V_BRG + 8] = I["b_rg"][0].T
    vecs[:, V_BIG:V_BIG + 8] = I["b_ig"][0].T
    vecs[:, V_LAM:V_LAM + 8] = col(I["lru_lambda"][0])
    wres = np.zeros((128, 2560), np.float32)
    wres[:, 0:512] = I["pool_w"][0].transpose(1, 0, 2).reshape(128, 512)
    wres[:, 512:1536] = I["w_rg"][0].transpose(1, 0, 2).reshape(128, 1024)
    wres[:, 1536:2560] = I["w_ig"][0].transpose(1, 0, 2).reshape(128, 1024)
    cbrow = np.ascontiguousarray(I["conv_b"][0].reshape(1, D))
    ident = np.eye(128, dtype=np.float32)
    maps = []
    for c in range(NCORES):
        b, q = divmod(c, 4)
        s0 = q * TPC
        xs = np.zeros((HALO + TPC, D), np.float32)
        xs[HALO:] = x[b, s0:s0 + TPC]
        if q > 0:
            xs[:HALO] = x[b, s0 - HALO:s0]
        xT = np.ascontiguousarray(xs.T.reshape(8, 128, HALO + TPC).transpose(1, 0, 2))
        pT = np.ascontiguousarray(p[b, s0:s0 + TPC].T.reshape(2, 128, TPC).transpose(1, 0, 2))
        pc = np.zeros((128, NPC), np.float32)
        pc[:, PC_FLAG] = 1.0 if q == 0 else 0.0
        pc[:, PC_OMF] = 0.0 if q == 0 else 1.0
        for r in range(NCORES):
            mval = 1.0 if (r // 4 == b and r % 4 < q) else 0.0
            pc[:, PC_M + r] = mval
            pc[:, PC_OMM + r] = 1.0 - mval
        maps.append({"xT": xT, "pT": pT, "wstream": wstream, "wres": wres, "vecs": vecs, "percore": pc,
                     "pmats": _pmats(q == 0), "ident": ident, "cbrow": cbrow})
    return maps


_NC_CACHE = {}


def kernel(**inputs):
    maps = _host_inputs(inputs)
    if "nc" not in _NC_CACHE:
        _NC_CACHE["nc"] = build()
    nc = _NC_CACHE["nc"]
    res = run_bass_kernel_spmd(nc, maps, core_ids=list(range(NCORES)))
    out = np.empty((2, SEQ, D), np.float32)
    for c in range(NCORES):
        b, q = divmod(c, 4)
        oT = np.asarray(res.results[c]["outT"])
        out[b, q * TPC:(q + 1) * TPC, :] = oT.transpose(2, 1, 0).reshape(TPC, D)
    return out
```
